# Optimizing a Trainium2 kernel written in Bass

```python
import math
import jax, jax.numpy as jnp
from jax import lax
import numpy as np

D_MODEL = 4096
BATCH = 2
SEQ = 8192
DEPTH = 1

A_HEADS = 32
A_KV_HEADS = 4
A_HEAD_DIM = 64
A_WINDOW = 128
B_HEADS = 16
B_KV_GROUPS = 2
B_HEAD_DIM = 128
CMP_BLOCK = 32
CMP_STRIDE = 16
CMP_HIDDEN = 128
SLC_BLOCK = 64
N_SELECT = 16
B_WINDOW = 512
BAND_BLOCK = 128
SLC_QBLOCK = 64
D_FF = 11008
PLE_DIM = 256
ROPE_THETA = 10000.0
EPS = 1e-6

COLS = [
    A_HEADS * A_HEAD_DIM,
    A_KV_HEADS * A_HEAD_DIM,
    A_KV_HEADS * A_HEAD_DIM,
    B_HEADS * B_HEAD_DIM,
    B_KV_GROUPS * B_HEAD_DIM,
    B_KV_GROUPS * B_HEAD_DIM,
    B_KV_GROUPS * B_HEAD_DIM,
    B_KV_GROUPS * B_HEAD_DIM,
    B_KV_GROUPS * B_HEAD_DIM,
    B_KV_GROUPS * B_HEAD_DIM,
    B_HEADS * 3,
    D_MODEL,
    D_MODEL,
]
D_IN = sum(COLS)
SPLITS = [int(c) for c in np.cumsum(COLS)[:-1]]

kernel_name = "hybrid_swa_sink_nsa_macaron_ple"


def rms_norm(x, g):
    xf = x.astype(jnp.float32)
    var = jnp.mean(xf * xf, axis=-1, keepdims=True)
    return (xf * lax.rsqrt(var + EPS) * g.astype(jnp.float32)).astype(x.dtype)


def swiglu(x, w_gate, w_up, w_down):
    return (jax.nn.silu(x @ w_gate) * (x @ w_up)) @ w_down


def rope(x, positions):
    d = x.shape[-1]
    inv = jnp.power(jnp.float32(ROPE_THETA), -jnp.arange(0, d, 2, dtype=jnp.float32) / d)
    ang = positions.astype(jnp.float32)[:, :, None] * inv
    cos = jnp.cos(ang)[:, :, None, :]
    sin = jnp.sin(ang)[:, :, None, :]
    xf = x.astype(jnp.float32)
    x1, x2 = xf[..., : d // 2], xf[..., d // 2:]
    return jnp.concatenate([x1 * cos - x2 * sin, x2 * cos + x1 * sin], axis=-1).astype(x.dtype)


def banded_attention(q, k, v, window, sinks=None):
    B, T, H, D = q.shape
    G = k.shape[2]
    R = H // G
    blk = BAND_BLOCK
    nb = T // blk
    P = -(-(window - 1) // blk)
    qb = q.reshape(B, nb, blk, G, R, D)
    pad = ((0, 0), (P * blk, 0), (0, 0), (0, 0))
    kb = jnp.pad(k, pad).reshape(B, nb + P, blk, G, D)
    vb = jnp.pad(v, pad).reshape(B, nb + P, blk, G, D)
    kw = jnp.concatenate([kb[:, i:i + nb] for i in range(P + 1)], axis=2)
    vw = jnp.concatenate([vb[:, i:i + nb] for i in range(P + 1)], axis=2)
    s = jnp.einsum('bnqgrd,bnkgd->bngrqk', qb, kw).astype(jnp.float32) * (D ** -0.5)
    kwin = (P + 1) * blk
    qpos = jnp.arange(blk)[:, None] + P * blk
    kpos = jnp.arange(kwin)[None, :]
    diff = qpos - kpos
    mask_rel = (diff >= 0) & (diff < window)
    mask_abs = (jnp.arange(nb)[:, None] * blk + kpos - P * blk) >= 0
    mask = mask_rel[None, :, :] & mask_abs[:, None, :]
    s = jnp.where(mask[None, :, None, None], s, -jnp.inf)
    if sinks is not None:
        sk = sinks.astype(jnp.float32).reshape(1, 1, G, R, 1, 1)
        m = jnp.maximum(jnp.max(s, axis=-1, keepdims=True), sk)
        e = jnp.exp(s - m)
        prob = e / (jnp.sum(e, axis=-1, keepdims=True) + jnp.exp(sk - m))
    else:
        prob = jax.nn.softmax(s, axis=-1)
    o = jnp.einsum('bngrqk,bnkgd->bnqgrd', prob.astype(v.dtype), vw)
    return o.reshape(B, T, H, D)


def compress(kv, pe, w1, w2):
    B, T, G, D = kv.shape
    nc = (T - CMP_BLOCK) // CMP_STRIDE + 1
    idx = jnp.arange(nc)[:, None] * CMP_STRIDE + jnp.arange(CMP_BLOCK)[None, :]
    blocks = kv[:, idx] + pe[None, None, :, None, :]
    flat = jnp.moveaxis(blocks, 3, 2).reshape(B, nc, G, CMP_BLOCK * D)
    return jax.nn.gelu(flat @ w1) @ w2


def selection_map(nc, ns):
    a, b = SLC_BLOCK // CMP_STRIDE, CMP_BLOCK // CMP_STRIDE
    j = np.arange(ns)[:, None, None]
    c = a * j - np.arange(a)[None, :, None] - np.arange(b)[None, None, :]
    jj = np.broadcast_to(j, c.shape)
    ok = (c >= 0) & (c < nc)
    mat = np.zeros((nc, ns), np.float32)
    np.add.at(mat, (c[ok], jj[ok]), 1.0)
    return jnp.asarray(mat)


def selected_attention(q, k, v, sel):
    B, T, H, D = q.shape
    G = k.shape[2]
    R = H // G
    n = sel.shape[-1]
    ns = T // SLC_BLOCK
    qbs = SLC_QBLOCK
    nqb = T // qbs
    kbt = k.reshape(B, ns, SLC_BLOCK, G, D).transpose(0, 3, 1, 2, 4)
    vbt = v.reshape(B, ns, SLC_BLOCK, G, D).transpose(0, 3, 1, 2, 4)
    bi = jnp.arange(B)[:, None, None, None]
    gi = jnp.arange(G)[None, :, None, None]
    q_blocks = q.reshape(B, nqb, qbs, G, R, D).transpose(1, 0, 2, 3, 4, 5)
    sel_blocks = sel.reshape(B, G, nqb, qbs, n).transpose(2, 0, 1, 3, 4)
    starts = jnp.arange(nqb, dtype=jnp.int32) * qbs
    scale = D ** -0.5

    def one_block(args):
        qblk, sblk, t0 = args
        kg = kbt[bi, gi, sblk]
        vg = vbt[bi, gi, sblk]
        s = jnp.einsum('bqgrd,bgqnld->bgrqnl', qblk, kg).astype(jnp.float32) * scale
        kpos = sblk[..., None] * SLC_BLOCK + jnp.arange(SLC_BLOCK)
        tpos = t0 + jnp.arange(qbs)
        mask = kpos <= tpos[None, None, :, None, None]
        s = jnp.where(mask[:, :, None], s, -jnp.inf)
        prob = jax.nn.softmax(s.reshape(B, G, R, qbs, n * SLC_BLOCK), axis=-1).reshape(s.shape)
        return jnp.einsum('bgrqnl,bgqnld->bqgrd', prob.astype(v.dtype), vg)

    o = lax.map(one_block, (q_blocks, sel_blocks, starts))
    return o.transpose(1, 0, 2, 3, 4, 5).reshape(B, T, H, D)


def nsa_attention(q, kc, vc, ks, vs, kw, vw, gates, positions,
                  pe_k, w_ck1, w_ck2, pe_v, w_cv1, w_cv2):
    B, T, H, D = q.shape
    G = kc.shape[2]
    R = H // G
    scale = D ** -0.5
    kcmp = compress(kc, pe_k, w_ck1, w_ck2)
    vcmp = compress(vc, pe_v, w_cv1, w_cv2)
    nc = kcmp.shape[1]
    qg = q.reshape(B, T, G, R, D)
    s = jnp.einsum('btgrd,bcgd->bgrtc', qg, kcmp).astype(jnp.float32) * scale
    t_idx = jnp.arange(T)
    visible = (jnp.arange(nc) * CMP_STRIDE + CMP_BLOCK - 1)[None, :] <= t_idx[:, None]
    s = jnp.where(visible, s, -jnp.inf)
    m = jnp.max(s, axis=-1, keepdims=True)
    m = jnp.where(jnp.isfinite(m), m, 0.0)
    e = jnp.exp(s - m)
    den = jnp.sum(e, axis=-1, keepdims=True)
    p_cmp = e / jnp.where(den > 0, den, 1.0)
    o_cmp = jnp.einsum('bgrtc,bcgd->btgrd', p_cmp.astype(vcmp.dtype), vcmp).reshape(B, T, H, D)
    ns = T // SLC_BLOCK
    n_sel = min(N_SELECT, ns)
    imp = jnp.einsum('bgrtc,cj->bgtj', p_cmp, selection_map(nc, ns))
    j = jnp.arange(ns)[None, :]
    cur = (t_idx // SLC_BLOCK)[:, None]
    valid = j * SLC_BLOCK <= t_idx[:, None]
    forced = (j == 0) | (j == cur) | (j == cur - 1)
    score = jnp.where(forced, jnp.inf, jnp.where(valid, imp, -jnp.inf))
    _, sel = lax.top_k(score, n_sel)
    qr = rope(q, positions)
    o_slc = selected_attention(qr, rope(ks, positions), vs, sel)
    o_win = banded_attention(qr, rope(kw, positions), vw, B_WINDOW)
    g = jax.nn.sigmoid(gates.astype(jnp.float32)).astype(q.dtype)
    return g[..., 0:1] * o_cmp + g[..., 1:2] * o_slc + g[..., 2:3] * o_win


def setup_inputs(seed: int = 0) -> dict:
    key = jax.random.key(seed)
    ks = jax.random.split(key, 32)

    def w(k, shape, fan_in):
        return jax.random.normal(k, shape, jnp.float32) * (fan_in ** -0.5)

    def gain(k, shape):
        return 1.0 + 0.02 * jax.random.normal(k, shape, jnp.float32)

    L = DEPTH
    x = jax.random.normal(ks[0], (BATCH, SEQ, D_MODEL), jnp.float32)
    p = jax.random.normal(ks[1], (DEPTH, BATCH, SEQ, PLE_DIM), jnp.float32)
    offs = jax.random.randint(ks[2], (BATCH, 1), 0, 4096, dtype=jnp.int32)
    positions = (jnp.arange(SEQ, dtype=jnp.int32)[None, :] + offs).astype(jnp.int32)
    return {
        "x": x,
        "p": p,
        "positions": positions,
        "ffn1_norm": gain(ks[3], (L, D_MODEL)),
        "ffn1_w_gate": w(ks[4], (L, D_MODEL, D_FF), D_MODEL),
        "ffn1_w_up": w(ks[5], (L, D_MODEL, D_FF), D_MODEL),
        "ffn1_w_down": w(ks[6], (L, D_FF, D_MODEL), D_FF),
        "mix_norm": gain(ks[7], (L, D_MODEL)),
        "w_in": w(ks[8], (L, D_MODEL, D_IN), D_MODEL),
        "sinks": jax.random.normal(ks[9], (L, A_HEADS), jnp.float32),
        "nsa_pe_k": 0.1 * jax.random.normal(ks[10], (L, CMP_BLOCK, B_HEAD_DIM), jnp.float32),
        "nsa_w_ck1": w(ks[11], (L, CMP_BLOCK * B_HEAD_DIM, CMP_HIDDEN), CMP_BLOCK * B_HEAD_DIM),
        "nsa_w_ck2": w(ks[12], (L, CMP_HIDDEN, B_HEAD_DIM), CMP_HIDDEN),
        "nsa_pe_v": 0.1 * jax.random.normal(ks[13], (L, CMP_BLOCK, B_HEAD_DIM), jnp.float32),
        "nsa_w_cv1": w(ks[14], (L, CMP_BLOCK * B_HEAD_DIM, CMP_HIDDEN), CMP_BLOCK * B_HEAD_DIM),
        "nsa_w_cv2": w(ks[15], (L, CMP_HIDDEN, B_HEAD_DIM), CMP_HIDDEN),
        "w_o_a": w(ks[16], (L, A_HEADS * A_HEAD_DIM, D_MODEL), A_HEADS * A_HEAD_DIM),
        "w_o_b": w(ks[17], (L, B_HEADS * B_HEAD_DIM, D_MODEL), B_HEADS * B_HEAD_DIM),
        "w_o": w(ks[18], (L, D_MODEL, D_MODEL), D_MODEL),
        "ffn2_norm": gain(ks[19], (L, D_MODEL)),
        "ffn2_w_gate": w(ks[20], (L, D_MODEL, D_FF), D_MODEL),
        "ffn2_w_up": w(ks[21], (L, D_MODEL, D_FF), D_MODEL),
        "ffn2_w_down": w(ks[22], (L, D_FF, D_MODEL), D_FF),
        "ple_norm": gain(ks[23], (L, D_MODEL)),
        "w_ple_gate": w(ks[24], (L, D_MODEL, D_MODEL), D_MODEL),
        "w_ple_proj": w(ks[25], (L, PLE_DIM, D_MODEL), PLE_DIM),
        "final_norm": gain(ks[26], (D_MODEL,)),
    }


def reference(x, p, positions, ffn1_norm, ffn1_w_gate, ffn1_w_up, ffn1_w_down,
              mix_norm, w_in, sinks, nsa_pe_k, nsa_w_ck1, nsa_w_ck2,
              nsa_pe_v, nsa_w_cv1, nsa_w_cv2, w_o_a, w_o_b, w_o,
              ffn2_norm, ffn2_w_gate, ffn2_w_up, ffn2_w_down,
              ple_norm, w_ple_gate, w_ple_proj, final_norm):
    B, T, _ = x.shape
    h = x
    for i in range(DEPTH):
        h = h + 0.5 * swiglu(rms_norm(h, ffn1_norm[i]), ffn1_w_gate[i], ffn1_w_up[i], ffn1_w_down[i])
        u = rms_norm(h, mix_norm[i])
        (qa, ka, va, qb, kc, vc, ksl, vsl, kwn, vwn, gb, gate_a, gate_b) = jnp.split(
            u @ w_in[i], SPLITS, axis=-1)
        qa = rope(qa.reshape(B, T, A_HEADS, A_HEAD_DIM), positions)
        ka = rope(ka.reshape(B, T, A_KV_HEADS, A_HEAD_DIM), positions)
        va = va.reshape(B, T, A_KV_HEADS, A_HEAD_DIM)
        o_a = banded_attention(qa, ka, va, A_WINDOW, sinks[i])
        kv_shape = (B, T, B_KV_GROUPS, B_HEAD_DIM)
        o_b = nsa_attention(
            qb.reshape(B, T, B_HEADS, B_HEAD_DIM),
            kc.reshape(kv_shape), vc.reshape(kv_shape),
            ksl.reshape(kv_shape), vsl.reshape(kv_shape),
            kwn.reshape(kv_shape), vwn.reshape(kv_shape),
            gb.reshape(B, T, B_HEADS, 3), positions,
            nsa_pe_k[i], nsa_w_ck1[i], nsa_w_ck2[i], nsa_pe_v[i], nsa_w_cv1[i], nsa_w_cv2[i])
        y_a = o_a.reshape(B, T, -1) @ w_o_a[i]
        y_b = o_b.reshape(B, T, -1) @ w_o_b[i]
        merged = jax.nn.sigmoid(gate_a) * y_a + jax.nn.sigmoid(gate_b) * y_b
        h = h + merged @ w_o[i]
        h = h + 0.5 * swiglu(rms_norm(h, ffn2_norm[i]), ffn2_w_gate[i], ffn2_w_up[i], ffn2_w_down[i])
        ple_gate = jax.nn.sigmoid(rms_norm(h, ple_norm[i]) @ w_ple_gate[i])
        h = h + (p[i] @ w_ple_proj[i]) * ple_gate
    return rms_norm(h, final_norm)
```

```python
import numpy as np
from contextlib import ExitStack
import ml_dtypes
import concourse.bass as bass
import concourse.mybir as mybir
from concourse.bass_utils import run_bass_kernel_spmd

F32 = mybir.dt.float32
BF16 = mybir.dt.bfloat16
I32 = mybir.dt.int32
AF = mybir.ActivationFunctionType
ALU = mybir.AluOpType
NPBF = ml_dtypes.bfloat16

D = 4096
DFF = 11008
NCORE = 8
TT = 512
EPS = 1e-6
NEG = -30000.0
PI = float(np.pi)


class Buf:
    __slots__ = ("ap", "writers", "readers", "dsem", "dcnt", "name")

    def __init__(self, ap, name=""):
        self.ap = ap
        self.writers = {}
        self.readers = {}
        self.dsem = None
        self.dcnt = 0
        self.name = name


class K:
    def __init__(self, nc, es):
        self.nc = nc
        self.es = es
        self.eng = {"pe": nc.tensor, "dve": nc.vector, "act": nc.scalar, "pool": nc.gpsimd, "sp": nc.sync}
        self.sem = {}
        self.cnt = {}
        self.semobj = {}
        for n in self.eng:
            s = es.enter_context(nc.semaphore("c_" + n))
            self.sem[n] = s
            self.cnt[n] = 0
            self.semobj[id(s)] = s
        self.waited = {n: {} for n in self.eng}
        self.nsem = 5
        self.dma_sems = []
        self.uid = 0

    def sb(self, shape, dt, name=None):
        self.uid += 1
        return self.es.enter_context(self.nc.sbuf_tensor(name or f"t{self.uid}", list(shape), dt))

    def ps(self, shape, dt, name=None):
        self.uid += 1
        return self.es.enter_context(self.nc.psum_tensor(name or f"p{self.uid}", list(shape), dt))

    def newsem(self, name):
        s = self.es.enter_context(self.nc.semaphore(name))
        self.semobj[id(s)] = s
        self.nsem += 1
        return s

    def buf(self, ap, name=""):
        return Buf(ap, name)

    def _wait(self, en, deps):
        e = self.eng[en]
        w = self.waited[en]
        for sid, val in deps.items():
            if en == "pe" and sid == id(self.sem["pe"]):
                continue
            if w.get(sid, 0) < val:
                e.wait_ge(self.semobj[sid], val)
                w[sid] = val

    @staticmethod
    def _merge(dst, src):
        for k, v in src.items():
            if dst.get(k, 0) < v:
                dst[k] = v

    def op(self, en, fn, reads=(), writes=()):
        deps = {}
        for b in reads:
            self._merge(deps, b.writers)
        for b in writes:
            self._merge(deps, b.writers)
            self._merge(deps, b.readers)
        self._wait(en, deps)
        ins = fn(self.eng[en])
        s = self.sem[en]
        ins.then_inc(s, 1)
        self.cnt[en] += 1
        tok = {id(s): self.cnt[en]}
        for b in reads:
            self._merge(b.readers, tok)
        for b in writes:
            b.writers = dict(tok)
            b.readers = {}
        return ins

    def dma(self, q, out_ap, in_ap, reads=(), writes=(), sembuf=None, **kw):
        deps = {}
        for b in reads:
            self._merge(deps, b.writers)
        for b in writes:
            self._merge(deps, b.writers)
            self._merge(deps, b.readers)
        self._wait(q, deps)
        owner = sembuf or (writes[0] if writes else reads[0])
        if owner.dsem is None:
            owner.dsem = self.newsem("d%d" % self.nsem)
            self.dma_sems.append(owner)
        ins = self.eng[q].dma_start(out=out_ap, in_=in_ap, **kw)
        ins.then_inc(owner.dsem, 16)
        owner.dcnt += 16
        tok = {id(owner.dsem): owner.dcnt}
        for b in reads:
            self._merge(b.readers, tok)
        for b in writes:
            self._merge(b.writers, tok)
            b.readers = {}
        return ins

    def finish(self, bufs):
        deps = {}
        for b in bufs:
            self._merge(deps, b.writers)
            self._merge(deps, b.readers)
        self._wait("sp", deps)


class Ring:
    def __init__(self, k, n, shape, dt, name):
        self.bufs = []
        for i in range(n):
            t = k.sb(shape, dt, f"{name}{i}")
            self.bufs.append(k.buf(t, f"{name}{i}"))
        self.i = 0

    def next(self):
        b = self.bufs[self.i % len(self.bufs)]
        self.i += 1
        return b


def _norm_to_bf16(k, hbufs, hT, gcol, xn, xnb, ones_f, ps_pool, scr):
    nc = k.nc
    sq_ring, rstd, rstd_b, tmp, tmp_b = scr
    pss = ps_pool.next()
    for c in range(32):
        sq = sq_ring.next()
        k.op("act", lambda e: e.activation(out=sq.ap[:], in_=hT[:, c, :], func=AF.Square),
             reads=[hbufs[c]], writes=[sq])
        k.op("pe", lambda e: e.matmul(pss.ap[:], lhsT=ones_f.ap[:], rhs=sq.ap[:], start=(c == 0), stop=(c == 31)),
             reads=[sq, ones_f], writes=[pss])
    k.op("dve", lambda e: e.tensor_scalar(out=tmp[:], in0=pss.ap[:], scalar1=1.0 / D, scalar2=EPS,
                                          op0=ALU.mult, op1=ALU.add), reads=[pss], writes=[tmp_b])
    k.op("act", lambda e: e.activation(out=tmp[:], in_=tmp[:], func=AF.Sqrt), reads=[tmp_b], writes=[tmp_b])
    k.op("dve", lambda e: e.reciprocal(out=rstd[:], in_=tmp[:]), reads=[tmp_b], writes=[rstd_b])
    for c in range(32):
        k.op("dve", lambda e: e.scalar_tensor_tensor(out=xn[:, c, :], in0=hT[:, c, :], scalar=gcol.ap[:, c:c + 1],
                                                  in1=rstd[:], op0=ALU.mult, op1=ALU.mult),
             reads=[hbufs[c], gcol, rstd_b], writes=[xnb[c]])


def _ffn(k, hT, hbufs, xn, xnb, wg, wu, wd, wring, ps_g, ps_u, ps_d, hid_ring, sg_ring):
    FG = 256
    for g in range(DFF // FG):
        f0 = g * FG
        wgb = wring.next()
        k.dma("pool", wgb.ap[:].rearrange("p (c f) -> p c f", f=FG),
              wg[:, f0:f0 + FG].rearrange("(c p) f -> p c f", p=128), writes=[wgb])
        wub = wring.next()
        k.dma("pool", wub.ap[:].rearrange("p (c f) -> p c f", f=FG),
              wu[:, f0:f0 + FG].rearrange("(c p) f -> p c f", p=128), writes=[wub])
        wdb = wring.next()
        k.dma("pool", wdb.ap[:].rearrange("p (c f) -> p c f", f=D),
              wd[f0:f0 + FG, :].rearrange("(c p) f -> p c f", p=128), writes=[wdb])
        wgv = wgb.ap[:].rearrange("p (c f) -> p c f", f=FG)
        wuv = wub.ap[:].rearrange("p (c f) -> p c f", f=FG)
        wdv = wdb.ap[:].rearrange("p (c f) -> p c f", f=D)
        hids = []
        for fc in range(2):
            pg = ps_g.next()
            pu = ps_u.next()
            for c in range(32):
                k.op("pe", lambda e: e.matmul(pg.ap[:], lhsT=wgv[:, c, fc * 128:(fc + 1) * 128], rhs=xn[:, c, :],
                                              start=(c == 0), stop=(c == 31)), reads=[wgb, xnb[c]], writes=[pg])
            for c in range(32):
                k.op("pe", lambda e: e.matmul(pu.ap[:], lhsT=wuv[:, c, fc * 128:(fc + 1) * 128], rhs=xn[:, c, :],
                                              start=(c == 0), stop=(c == 31)), reads=[wub, xnb[c]], writes=[pu])
            sg = sg_ring.next()
            k.op("act", lambda e: e.activation(out=sg.ap[:], in_=pg.ap[:], func=AF.Silu), reads=[pg], writes=[sg])
            hid = hid_ring.next()
            k.op("dve", lambda e: e.tensor_tensor(out=hid.ap[:], in0=pu.ap[:], in1=sg.ap[:], op=ALU.mult),
                 reads=[pu, sg], writes=[hid])
            hids.append(hid)
        for dc in range(32):
            pd = ps_d.next()
            for fc in range(2):
                k.op("pe", lambda e: e.matmul(pd.ap[:], lhsT=wdv[:, fc, dc * 128:(dc + 1) * 128], rhs=hids[fc].ap[:],
                                              start=(fc == 0), stop=(fc == 1)), reads=[wdb, hids[fc]], writes=[pd])
            k.op("dve", lambda e: e.scalar_tensor_tensor(out=hT[:, dc, :], in0=pd.ap[:], scalar=0.5, in1=hT[:, dc, :],
                                                         op0=ALU.mult, op1=ALU.add),
                 reads=[pd, hbufs[dc]], writes=[hbufs[dc]])


class PsRing:
    def __init__(self, k, n, name, shape=(128, 512), dt=F32):
        self.bufs = [k.buf(k.ps(shape, dt, f"{name}{i}"), f"{name}{i}") for i in range(n)]
        self.i = 0

    def next(self):
        b = self.bufs[self.i % len(self.bufs)]
        self.i += 1
        return b


NTOK = 2048


def build_A(ntiles=NTOK // TT, dbg=False):
    nc = bass.Bass("TRN2", target_bir_lowering=False)
    ntok = ntiles * TT
    dt_ = lambda n, s, d, kind: nc.dram_tensor(n, list(s), d, kind=kind).ap()
    xT = dt_("xT", [D, ntok], F32, "ExternalInput")
    pos = dt_("pos", [1, ntok], I32, "ExternalInput")
    g1 = dt_("g1", [128, 32], F32, "ExternalInput")
    g2 = dt_("g2", [128, 32], F32, "ExternalInput")
    wg = dt_("wg", [D, DFF], F32, "ExternalInput")
    wu = dt_("wu", [D, DFF], F32, "ExternalInput")
    wd = dt_("wd", [DFF, D], F32, "ExternalInput")
    win = dt_("win", [D, 14384], F32, "ExternalInput")
    rmat = dt_("rmat", [128, 256], F32, "ExternalInput")
    invf = dt_("invf", [128, 2], F32, "ExternalInput")
    h1T = dt_("h1T", [D, ntok], F32, "ExternalOutput")
    qaT = dt_("qaT", [2048, ntok], BF16, "ExternalOutput")
    qbT = dt_("qbT", [2048, ntok], BF16, "ExternalOutput")
    qbrT = dt_("qbrT", [2048, ntok], BF16, "ExternalOutput")
    kT = dt_("kT", [1280, ntok], BF16, "ExternalOutput")
    vtok = dt_("vtok", [ntok, 768], BF16, "ExternalOutput")
    gbT = dt_("gbT", [48, ntok], F32, "ExternalOutput")

    with ExitStack() as es:
        k = K(nc, es)
        hT = k.sb([128, 32, TT], F32, "hT")
        hbufs = [k.buf(hT[:, c, :]) for c in range(32)]
        xn = k.sb([128, 32, TT], BF16, "xn")
        xnb = [k.buf(xn[:, c, :]) for c in range(32)]
        wring = Ring(k, 4, [128, 8192], BF16, "w")
        ones_f = k.buf(k.sb([128, 128], F32, "ones"))
        g1b = k.buf(k.sb([128, 32], F32, "g1s"))
        g2b = k.buf(k.sb([128, 32], F32, "g2s"))
        rm = k.buf(k.sb([128, 256], F32, "rm"))
        ivf = k.buf(k.sb([128, 2], F32, "ivf"))
        f32r = Ring(k, 6, [128, TT], F32, "f32r")
        sq_ring = f32r
        rstd = k.sb([128, TT], F32, "rstd"); rstd_b = k.buf(rstd)
        tmp = k.sb([128, TT], F32, "tmp"); tmp_b = k.buf(tmp)
        hid_ring = Ring(k, 4, [128, TT], BF16, "hid")
        sg_ring = f32r
        posi = k.sb([128, TT], I32, "posi"); posi_b = k.buf(posi)
        ang = k.sb([128, TT], F32, "ang"); ang_b = k.buf(ang)
        trig = k.sb([128, 4, TT], F32, "trig")
        trig_b = [k.buf(trig[:, i, :]) for i in range(4)]
        xs_ring = f32r
        t1_ring = f32r
        t2_ring = f32r
        ob_ring = Ring(k, 4, [128, TT], BF16, "ob")
        of_ring = f32r
        vt_ring = Ring(k, 2, [128, 256], BF16, "vt")
        ps_g = PsRing(k, 2, "psg")
        ps_u = PsRing(k, 2, "psu")
        ps_d = PsRing(k, 4, "psd")
        scr = (sq_ring, rstd, rstd_b, tmp, tmp_b)
        outs = [k.buf(a) for a in (h1T, qaT, qbT, qbrT, kT, vtok, gbT)]
        o_h1, o_qa, o_qb, o_qbr, o_k, o_v, o_gb = outs

        k.op("dve", lambda e: e.memset(ones_f.ap[:], 1.0), writes=[ones_f])
        k.dma("sp", g1b.ap[:], g1, writes=[g1b])
        k.dma("sp", g2b.ap[:], g2, writes=[g2b])
        k.dma("sp", rm.ap[:], rmat, writes=[rm])
        k.dma("sp", ivf.ap[:], invf, writes=[ivf])

        for t in range(ntiles):
            ts = slice(t * TT, (t + 1) * TT)
            for c4 in range(4):
                k.dma("sp", hT[:, c4 * 8:(c4 + 1) * 8, :],
                      xT[c4 * 1024:(c4 + 1) * 1024, ts].rearrange("(c p) t -> p c t", p=128),
                      writes=hbufs[c4 * 8:(c4 + 1) * 8], sembuf=hbufs[c4 * 8])
            _norm_to_bf16(k, hbufs, hT, g1b, xn, xnb, ones_f, ps_d, scr)
            _ffn(k, hT, hbufs, xn, xnb, wg, wu, wd, wring, ps_g, ps_u, ps_d, hid_ring, sg_ring)
            for c4 in range(4):
                k.dma("sp", h1T[c4 * 1024:(c4 + 1) * 1024, ts].rearrange("(c p) t -> p c t", p=128),
                      hT[:, c4 * 8:(c4 + 1) * 8, :], reads=hbufs[c4 * 8:(c4 + 1) * 8], writes=[o_h1], sembuf=o_h1)
            _norm_to_bf16(k, hbufs, hT, g2b, xn, xnb, ones_f, ps_d, scr)
            k.dma("sp", posi[:], pos[0:1, ts].to_broadcast([128, TT]),
                  writes=[posi_b])
            k.op("dve", lambda e: e.tensor_copy(out=ang[:], in_=posi[:]), reads=[posi_b], writes=[ang_b])
            for hd in range(2):
                for cs in range(2):
                    tb = trig_b[hd * 2 + cs]
                    shift = (PI / 2 if cs == 0 else 0.0)
                    k.op("dve", lambda e: e.tensor_scalar(out=tmp[:], in0=ang[:], scalar1=ivf.ap[:, hd:hd + 1],
                                                          scalar2=shift, op0=ALU.mult, op1=ALU.add),
                         reads=[ang_b, ivf], writes=[tmp_b])
                    k.op("dve", lambda e: e.tensor_scalar(out=posi[:], in0=tmp[:], scalar1=1.0 / (2 * PI), scalar2=None,
                                                          op0=ALU.mult), reads=[tmp_b], writes=[posi_b])
                    k.op("dve", lambda e: e.tensor_copy(out=rstd[:], in_=posi[:]), reads=[posi_b], writes=[rstd_b])
                    k.op("dve", lambda e: e.scalar_tensor_tensor(out=tmp[:], in0=rstd[:], scalar=-2 * PI, in1=tmp[:],
                                                                 op0=ALU.mult, op1=ALU.add),
                         reads=[rstd_b, tmp_b], writes=[tmp_b])
                    k.op("dve", lambda e: e.tensor_scalar(out=rstd[:], in0=tmp[:], scalar1=PI, scalar2=-2 * PI,
                                                          op0=ALU.is_gt, op1=ALU.mult), reads=[tmp_b], writes=[rstd_b])
                    k.op("dve", lambda e: e.tensor_tensor(out=tmp[:], in0=tmp[:], in1=rstd[:], op=ALU.add),
                         reads=[tmp_b, rstd_b], writes=[tmp_b])
                    k.op("dve", lambda e: e.tensor_scalar(out=tmp[:], in0=tmp[:], scalar1=-PI, scalar2=PI,
                                                          op0=ALU.max, op1=ALU.min), reads=[tmp_b], writes=[tmp_b])
                    k.op("act", lambda e: e.activation(out=tb.ap, in_=tmp[:], func=AF.Sin), reads=[tmp_b], writes=[tb])

            def fm_chunk(wb, wv, j, mode, dst, dst_b, row0):
                pp = ps_g.next() if j == 0 else ps_u.next()
                for c in range(32):
                    k.op("pe", lambda e: e.matmul(pp.ap[:], lhsT=wv[:, c, j * 128:(j + 1) * 128], rhs=xn[:, c, :],
                                                  start=(c == 0), stop=(c == 31)), reads=[wb, xnb[c]], writes=[pp])
                for (m, dd, ddb) in mode_list(mode, dst, dst_b):
                    if m == "plain":
                        ob = ob_ring.next()
                        k.op("act", lambda e: e.copy(out=ob.ap[:], in_=pp.ap[:]), reads=[pp], writes=[ob])
                        k.dma("sp", dd[row0:row0 + 128, ts], ob.ap[:], reads=[ob], writes=[ddb], sembuf=ob)
                    else:
                        hd = 0 if m == "rope128" else 1
                        xs = xs_ring.next()
                        k.op("act", lambda e: e.copy(out=xs.ap[:], in_=pp.ap[:]), reads=[pp], writes=[xs])
                        pr = ps_d.next()
                        k.op("pe", lambda e: e.matmul(pr.ap[:], lhsT=rm.ap[:, hd * 128:(hd + 1) * 128], rhs=xs.ap[:],
                                                      start=True, stop=True), reads=[rm, xs], writes=[pr])
                        t1 = t1_ring.next()
                        k.op("pool", lambda e: e.tensor_tensor(out=t1.ap[:], in0=xs.ap[:], in1=trig[:, hd * 2, :],
                                                               op=ALU.mult), reads=[xs, trig_b[hd * 2]], writes=[t1])
                        t2 = t2_ring.next()
                        k.op("dve", lambda e: e.tensor_tensor(out=t2.ap[:], in0=pr.ap[:], in1=trig[:, hd * 2 + 1, :],
                                                              op=ALU.mult), reads=[pr, trig_b[hd * 2 + 1]], writes=[t2])
                        ob = ob_ring.next()
                        k.op("dve", lambda e: e.tensor_tensor(out=ob.ap[:], in0=t1.ap[:], in1=t2.ap[:], op=ALU.add),
                             reads=[t1, t2], writes=[ob])
                        k.dma("sp", dd[row0:row0 + 128, ts], ob.ap[:], reads=[ob], writes=[ddb], sembuf=ob)

            def mode_list(mode, dst, dst_b):
                if mode == "qb":
                    return [("plain", qbT, o_qb), ("rope128", qbrT, o_qbr)]
                return [(mode, dst, dst_b)]

            blocks = []
            for b in range(8):
                blocks.append((b * 256, "fm", "rope64", qaT, o_qa, b * 256))
            blocks.append((2048, "fm", "rope64", kT, o_k, 0))
            blocks.append((2304, "tm", None, None, None, 0))
            for b in range(8):
                blocks.append((2560 + b * 256, "fm", "qb", None, None, b * 256))
            blocks.append((4608, "fm", "plain", kT, o_k, 256))
            blocks.append((4864, "fm", "plain", kT, o_k, 512))
            blocks.append((5120, "fm", "rope128", kT, o_k, 768))
            blocks.append((5376, "tm", None, None, None, 256))
            blocks.append((5632, "fm", "rope128", kT, o_k, 1024))
            blocks.append((5888, "tm", None, None, None, 512))
            for (c0, kind, mode, dst, dst_b, row0) in blocks:
                wb = wring.next()
                wv = wb.ap[:].rearrange("p (c f) -> p c f", f=256)
                k.dma("pool", wv, win[:, c0:c0 + 256].rearrange("(c p) f -> p c f", p=128), writes=[wb])
                if kind == "fm":
                    for j in range(2):
                        fm_chunk(wb, wv, j, mode, dst, dst_b, row0 + j * 128)
                else:
                    for s in range(4):
                        pp = ps_g.next() if s % 2 == 0 else ps_u.next()
                        for c in range(32):
                            k.op("pe", lambda e: e.matmul(pp.ap[:, 0:256], lhsT=xn[:, c, s * 128:(s + 1) * 128],
                                                          rhs=wv[:, c, :], start=(c == 0), stop=(c == 31)),
                                 reads=[wb, xnb[c]], writes=[pp])
                        vt = vt_ring.next()
                        k.op("act", lambda e: e.copy(out=vt.ap[:], in_=pp.ap[:, 0:256]), reads=[pp], writes=[vt])
                        k.dma("sp", vtok[t * TT + s * 128:t * TT + (s + 1) * 128, row0:row0 + 256], vt.ap[:],
                              reads=[vt], writes=[o_v], sembuf=vt)
            wb = wring.next()
            wv = wb.ap[:, 0:32 * 48].rearrange("p (c f) -> p c f", f=48)
            k.dma("pool", wv, win[:, 6144:6192].rearrange("(c p) f -> p c f", p=128), writes=[wb])
            pp = ps_g.next()
            for c in range(32):
                k.op("pe", lambda e: e.matmul(pp.ap[0:48, :], lhsT=wv[:, c, :], rhs=xn[:, c, :],
                                              start=(c == 0), stop=(c == 31)), reads=[wb, xnb[c]], writes=[pp])
            of = of_ring.next()
            k.op("act", lambda e: e.activation(out=of.ap[0:48, :], in_=pp.ap[0:48, :], func=AF.Sigmoid),
                 reads=[pp], writes=[of])
            k.dma("sp", gbT[:, ts], of.ap[0:48, :], reads=[of], writes=[o_gb], sembuf=of)
        k.finish(outs + ob_ring.bufs + vt_ring.bufs + f32r.bufs)
        stats = dict(k.cnt)
    return nc, stats


def _consts_A():
    R128 = np.zeros((128, 128), np.float32)
    for m in range(128):
        if m < 64:
            R128[m + 64, m] = -1.0
        else:
            R128[m - 64, m] = 1.0
    R64 = np.zeros((128, 128), np.float32)
    for m in range(128):
        if (m % 64) < 32:
            R64[m + 32, m] = -1.0
        else:
            R64[m - 32, m] = 1.0
    rmat = np.concatenate([R128, R64], axis=1)
    inv128 = np.power(np.float32(10000.0), -np.arange(0, 128, 2, dtype=np.float32) / np.float32(128))
    inv64 = np.power(np.float32(10000.0), -np.arange(0, 64, 2, dtype=np.float32) / np.float32(64))
    invf = np.zeros((128, 2), np.float32)
    for p in range(128):
        invf[p, 0] = inv128[p % 64]
        invf[p, 1] = inv64[p % 32]
    return rmat, invf


def gcol(g):
    return np.ascontiguousarray(g.reshape(32, 128).T.astype(np.float32))


SEQ = 8192
NM = 26
M_P, M_P3, M_L, M_WH, M_WO, M_AH, M_AO = 0, 5, 9, 13, 17, 21, 22


def build_B(nqt=SEQ // TT):
    nc = bass.Bass("TRN2", target_bir_lowering=False)
    S = nqt * TT
    NKT = S // 128
    dt_ = lambda n, s, d, kind="ExternalInput": nc.dram_tensor(n, list(s), d, kind=kind).ap()
    qa = dt_("qa", [512, S], BF16)
    ka2 = dt_("ka2", [128, S], BF16)
    va = dt_("va", [S, 64], BF16)
    esk = dt_("esk", [128, 4], F32)
    qb = dt_("qb", [1024, S], BF16)
    qbr = dt_("qbr", [512, S], BF16)
    kc = dt_("kc", [128, S], BF16)
    vc = dt_("vc", [128, S], BF16)
    ks = dt_("ks", [128, S], BF16)
    vs = dt_("vs", [S, 128], BF16)
    kw = dt_("kw", [128, S], BF16)
    vw = dt_("vw", [S, 128], BF16)
    gb = dt_("gb", [12, S], F32)
    pekT = dt_("pekT", [128, 32], F32)
    pevT = dt_("pevT", [128, 32], F32)
    wk1 = dt_("wk1", [4096, 128], F32)
    wk2 = dt_("wk2", [128, 128], F32)
    wv1 = dt_("wv1", [4096, 128], F32)
    wv2 = dt_("wv2", [128, 128], F32)
    masks = dt_("masks", [128, NM * 512], BF16)
    selmap = dt_("selmap", [128, 4 * 128], BF16)
    selbias = dt_("selbias", [SEQ, 128], F32)
    farneg = dt_("farneg", [SEQ, 128], F32)
    emat = dt_("emat", [128, 64 * 128], BF16)
    cst = dt_("cst", [128, 128 + 128 + 192], BF16)
    identf = dt_("identf", [128, 128], F32)
    oa = dt_("oa", [512, S], BF16, "ExternalOutput")
    ob = dt_("ob", [512, S], BF16, "ExternalOutput")
    SC128 = 128 ** -0.5
    SC64 = 64 ** -0.5

    with ExitStack() as es:
        k = K(nc, es)
        sbb = lambda shape, dt, name: k.buf(k.sb(shape, dt, name), name)
        ksT = sbb([128, S], BF16, "ksT")
        vsS = sbb([128, NKT, 128], BF16, "vsS")
        kwT = sbb([128, S], BF16, "kwT")
        vwS = sbb([128, NKT * 128], BF16, "vwS")
        kaT = sbb([128, S], BF16, "kaT")
        vaP = sbb([128, NKT, 192], BF16, "vaP")
        mk = sbb([128, NM, 512], BF16, "mk")
        em = sbb([128, 64, 128], BF16, "em")
        smap = sbb([128, 4, 128], BF16, "smap")
        cs = sbb([128, 448], BF16, "cs")
        idf = sbb([128, 128], F32, "idf")
        esink = sbb([128, 4], F32, "esink")
        kcmpT = sbb([128, 512], BF16, "kcmpT")
        vcmp = sbb([128, 4, 128], BF16, "vcmp")
        w1 = sbb([128, 32, 128], BF16, "w1")
        w2 = sbb([128, 128], BF16, "w2")
        peT = sbb([128, 32], BF16, "peT")
        bias = sbb([128, 1], F32, "bias")
        gT = sbb([128, 512], BF16, "gT")
        ident = cs.ap[:, 0:128]
        ones = cs.ap[:, 128:256]
        onesA = cs.ap[:, 320:448]
        onesB = cs.ap[:, 256:384]
        f32r = Ring(k, 5, [128, 512], F32, "f32r")
        q_ring = Ring(k, 4, [128, 512], BF16, "q")
        p_ring = Ring(k, 5, [128, 512], BF16, "pt")
        nselT = sbb([128, 512], BF16, "nselT")
        sbias = sbb([128, 4, 128], F32, "sbias")
        fneg = sbb([128, 4, 128], F32, "fneg")
        impS = sbb([128, 512], F32, "impS")
        sc = sbb([128, 128], F32, "sc")
        sc2 = sbb([128, 128], F32, "sc2")
        m8 = sbb([128, 8], F32, "m8")
        nsl = sbb([128, 128], F32, "nsl")
        acc = [sbb([128, 512], F32, f"acc{i}") for i in range(4)]
        g_ring = Ring(k, 2, [128, 512], F32, "g")
        o_ring = Ring(k, 2, [128, 512], BF16, "o")
        ps_s = PsRing(k, 3, "pss")
        ps_o = PsRing(k, 2, "pso")
        ps_den = PsRing(k, 2, "psden")
        ps_m = PsRing(k, 1, "psm")
        o_oa = k.buf(oa); o_ob = k.buf(ob)

        k.dma("sp", cs.ap[:], cst, writes=[cs])
        k.dma("sp", idf.ap[:], identf, writes=[idf])
        k.dma("sp", mk.ap[:].rearrange("p m f -> p (m f)"), masks, writes=[mk])
        k.dma("sp", em.ap[:].rearrange("p m f -> p (m f)"), emat, writes=[em])
        k.dma("sp", smap.ap[:].rearrange("p m f -> p (m f)"), selmap, writes=[smap])
        k.dma("sp", esink.ap[:], esk, writes=[esink])
        k.op("act", lambda e: e.activation(out=esink.ap[:], in_=esink.ap[:], func=AF.Exp), reads=[esink], writes=[esink])
        k.dma("sp", ksT.ap[:], ks, writes=[ksT])
        k.dma("sp", vsS.ap[:], vs.rearrange("(t p) d -> p t d", p=128), writes=[vsS])
        k.dma("sp", kaT.ap[:], ka2, writes=[kaT])
        k.op("pool", lambda e: e.memset(vaP.ap[:], 0.0), writes=[vaP])
        k.dma("sp", vaP.ap[:, :, 64:128], va.rearrange("(t p) d -> p t d", p=128), writes=[vaP])
        k.dma("sp", kwT.ap[:], kc, writes=[kwT])
        k.dma("sp", vwS.ap[:], vc, writes=[vwS])
        k.op("dve", lambda e: e.memset(gT.ap[:], 0.0), writes=[gT])
        k.op("dve", lambda e: e.memset(kcmpT.ap[:], 0.0), writes=[kcmpT])

        NCB = S // 16 - 1
        for which, (srcb, peD, w1D, w2D) in enumerate(((kwT, pekT, wk1, wk2), (vwS, pevT, wv1, wv2))):
            k.dma("pool", w1.ap[:], w1D.rearrange("(l d) h -> d l h", d=128), writes=[w1])
            k.dma("pool", w2.ap[:], w2D, writes=[w2])
            k.dma("pool", peT.ap[:], peD, writes=[peT])
            pb = ps_m.next()
            for l in range(32):
                k.op("pe", lambda e: e.matmul(pb.ap[:, 0:1], lhsT=w1.ap[:, l, :], rhs=peT.ap[:, l:l + 1],
                                              start=(l == 0), stop=(l == 31)), reads=[w1, peT], writes=[pb])
            k.op("dve", lambda e: e.tensor_copy(out=bias.ap[:], in_=pb.ap[:, 0:1]), reads=[pb], writes=[bias])
            ph = ps_s.next()
            srcv = srcb.ap[:].rearrange("p (c s) -> p c s", s=16)
            for l in range(32):
                a, r = l // 16, l % 16
                k.op("pe", lambda e: e.matmul(ph.ap[:, 0:NCB], lhsT=w1.ap[:, l, :], rhs=srcv[:, a:a + NCB, r],
                                              start=(l == 0), stop=(l == 31)), reads=[w1, srcb], writes=[ph])
            x = f32r.next(); x2 = f32r.next(); sg = f32r.next()
            k.op("dve", lambda e: e.tensor_scalar(out=x.ap[:, 0:NCB], in0=ph.ap[:, 0:NCB], scalar1=bias.ap[:, 0:1],
                                                  scalar2=None, op0=ALU.add), reads=[ph, bias], writes=[x])
            k.op("dve", lambda e: e.tensor_tensor(out=x2.ap[:, 0:NCB], in0=x.ap[:, 0:NCB], in1=x.ap[:, 0:NCB], op=ALU.mult),
                 reads=[x], writes=[x2])
            k.op("dve", lambda e: e.tensor_scalar(out=x2.ap[:, 0:NCB], in0=x2.ap[:, 0:NCB], scalar1=0.044715, scalar2=1.0,
                                                  op0=ALU.mult, op1=ALU.add), reads=[x2], writes=[x2])
            k.op("dve", lambda e: e.tensor_tensor(out=x2.ap[:, 0:NCB], in0=x2.ap[:, 0:NCB], in1=x.ap[:, 0:NCB], op=ALU.mult),
                 reads=[x2, x], writes=[x2])
            k.op("act", lambda e: e.activation(out=sg.ap[:, 0:NCB], in_=x2.ap[:, 0:NCB], func=AF.Sigmoid,
                                               scale=1.5957691216057308), reads=[x2], writes=[sg])
            k.op("dve", lambda e: e.tensor_tensor(out=gT.ap[:, 0:NCB], in0=x.ap[:, 0:NCB], in1=sg.ap[:, 0:NCB], op=ALU.mult),
                 reads=[x, sg], writes=[gT])
            if which == 0:
                pk = ps_s.next()
                k.op("pe", lambda e: e.matmul(pk.ap[:, 0:NCB], lhsT=w2.ap[:], rhs=gT.ap[:, 0:NCB], start=True, stop=True),
                     reads=[w2, gT], writes=[pk])
                k.op("act", lambda e: e.copy(out=kcmpT.ap[:, 0:NCB], in_=pk.ap[:, 0:NCB]), reads=[pk], writes=[kcmpT])
            else:
                for ct in range((NCB + 127) // 128):
                    pk = ps_s.next()
                    k.op("pe", lambda e: e.matmul(pk.ap[:, 0:128], lhsT=gT.ap[:, ct * 128:(ct + 1) * 128], rhs=w2.ap[:],
                                                  start=True, stop=True), reads=[w2, gT], writes=[pk])
                    k.op("act", lambda e: e.copy(out=vcmp.ap[:, ct, :], in_=pk.ap[:, 0:128]), reads=[pk], writes=[vcmp])
        k.dma("sp", kwT.ap[:], kw, writes=[kwT])
        k.dma("sp", vwS.ap[:].rearrange("p (t d) -> p t d", d=128), vw.rearrange("(t p) d -> p t d", p=128), writes=[vwS])
        vwv = vwS.ap[:].rearrange("p (t d) -> p t d", d=128)

        def load_q(src, row0, ts):
            qt_ = q_ring.next()
            k.dma("sp", qt_.ap[:], src[row0:row0 + 128, ts], writes=[qt_])
            return qt_

        def load_gate(row, ts):
            g = g_ring.next()
            k.dma("sp", g.ap[:], gb[row:row + 1, ts].to_broadcast([128, 512]), writes=[g])
            return g

        def attn_pairs(qtile, qrows, pairs, scale, po, pden, den_lhsT, first):
            n = len(pairs)
            for i, (kap, kbuf, ml, mr, mb, vl, vbuf) in enumerate(pairs):
                pS = ps_s.next()
                k.op("pe", lambda e: e.matmul(pS.ap[:], lhsT=kap, rhs=qrows, start=True, stop=(ml is None)),
                     reads=[kbuf, qtile], writes=[pS])
                if ml is not None:
                    k.op("pe", lambda e: e.matmul(pS.ap[:], lhsT=ml, rhs=mr, start=False, stop=True),
                         reads=list(mb), writes=[pS])
                pt = p_ring.next()
                k.op("act", lambda e: e.activation(out=pt.ap[:], in_=pS.ap[:], func=AF.Exp, scale=scale),
                     reads=[pS], writes=[pt])
                st = first and (i == 0)
                k.op("pe", lambda e: e.matmul(po.ap[:], lhsT=vl, rhs=pt.ap[:], start=st, stop=False, skip_group_check=True),
                     reads=[vbuf, pt], writes=[po])
                k.op("pe", lambda e: e.matmul(pden.ap[:], lhsT=den_lhsT, rhs=pt.ap[:], start=st, stop=False,
                                              skip_group_check=True), reads=[cs, pt], writes=[pden])

        for qt in range(nqt):
            ts = slice(qt * TT, (qt + 1) * TT)
            k.dma("sp", sbias.ap[:], selbias[ts, :].rearrange("(s p) j -> p s j", p=128), writes=[sbias])
            k.dma("sp", fneg.ap[:], farneg[ts, :].rearrange("(s p) j -> p s j", p=128), writes=[fneg])
            ncts = min(3, qt // 4) + 1
            pimp = ps_m.next()
            for h in range(8):
                qtile = load_q(qb, h * 128, ts)
                own = None
                pden = ps_den.next()
                ets = []
                for ct in range(ncts):
                    rel = qt - 4 * ct
                    pS = ps_s.next()
                    mi = None
                    if ct == 3:
                        mi = M_P3 + rel
                    elif rel < 5:
                        mi = M_P + rel
                    k.op("pe", lambda e: e.matmul(pS.ap[:], lhsT=kcmpT.ap[:, ct * 128:(ct + 1) * 128], rhs=qtile.ap[:],
                                                  start=True, stop=(mi is None)), reads=[kcmpT, qtile], writes=[pS])
                    if mi is not None:
                        k.op("pe", lambda e: e.matmul(pS.ap[:], lhsT=ident, rhs=mk.ap[:, mi, :], start=False, stop=True),
                             reads=[cs, mk], writes=[pS])
                    et = p_ring.next()
                    k.op("act", lambda e: e.activation(out=et.ap[:], in_=pS.ap[:], func=AF.Exp, scale=SC128),
                         reads=[pS], writes=[et])
                    k.op("pe", lambda e: e.matmul(pden.ap[:], lhsT=ones, rhs=et.ap[:], start=(ct == 0), stop=(ct == ncts - 1)),
                         reads=[cs, et], writes=[pden])
                    ets.append(et)
                rd = f32r.next()
                k.op("dve", lambda e: e.tensor_scalar(out=rd.ap[:], in0=pden.ap[:], scalar1=1e-30, scalar2=None, op0=ALU.add),
                     reads=[pden], writes=[rd])
                k.op("dve", lambda e: e.reciprocal(out=rd.ap[:], in_=rd.ap[:]), reads=[rd], writes=[rd])
                for ct in range(ncts):
                    k.op("dve", lambda e: e.tensor_tensor(out=ets[ct].ap[:], in0=ets[ct].ap[:], in1=rd.ap[:], op=ALU.mult),
                         reads=[ets[ct], rd], writes=[ets[ct]])
                for ct in range(ncts):
                    k.op("pe", lambda e: e.matmul(pimp.ap[:], lhsT=smap.ap[:, ct, :], rhs=ets[ct].ap[:],
                                                  start=(h == 0 and ct == 0), stop=(h == 7 and ct == ncts - 1),
                                                  skip_group_check=True), reads=[smap, ets[ct]], writes=[pimp])
                if h < 4:
                    po = ps_o.next()
                    for ct in range(ncts):
                        k.op("pe", lambda e: e.matmul(po.ap[:], lhsT=vcmp.ap[:, ct, :], rhs=ets[ct].ap[:],
                                                      start=(ct == 0), stop=(ct == ncts - 1)), reads=[vcmp, ets[ct]], writes=[po])
                    g0 = load_gate(h * 3 + 0, ts)
                    k.op("dve", lambda e: e.tensor_tensor(out=acc[h].ap[:], in0=po.ap[:], in1=g0.ap[:], op=ALU.mult),
                         reads=[po, g0], writes=[acc[h]])
            k.op("act", lambda e: e.copy(out=impS.ap[:], in_=pimp.ap[:]), reads=[pimp], writes=[impS])
            for s4 in range(4):
                ptr = ps_s.next()
                k.op("pe", lambda e: e.transpose(out=ptr.ap[:, 0:128], in_=impS.ap[:, s4 * 128:(s4 + 1) * 128], identity=idf.ap[:]),
                     reads=[impS, idf], writes=[ptr])
                k.op("dve", lambda e: e.tensor_tensor(out=sc.ap[:], in0=ptr.ap[:, 0:128], in1=sbias.ap[:, s4, :], op=ALU.add),
                     reads=[ptr, sbias], writes=[sc])
                k.op("dve", lambda e: e.max(out=m8.ap[:], in_=sc.ap[:]), reads=[sc], writes=[m8])
                k.op("dve", lambda e: e.match_replace(out=sc2.ap[:], in_to_replace=m8.ap[:], in_values=sc.ap[:], imm_value=-3.0e38),
                     reads=[sc, m8], writes=[sc2])
                k.op("dve", lambda e: e.max(out=m8.ap[:], in_=sc2.ap[:]), reads=[sc2], writes=[m8])
                k.op("dve", lambda e: e.tensor_scalar(out=sc2.ap[:], in0=sc.ap[:], scalar1=m8.ap[:, 7:8], scalar2=None, op0=ALU.is_ge),
                     reads=[sc, m8], writes=[sc2])
                k.op("dve", lambda e: e.tensor_scalar(out=sc2.ap[:], in0=sc2.ap[:], scalar1=-1.0, scalar2=-NEG, op0=ALU.add, op1=ALU.mult),
                     reads=[sc2], writes=[sc2])
                k.op("dve", lambda e: e.tensor_tensor(out=nsl.ap[:], in0=sc2.ap[:], in1=fneg.ap[:, s4, :], op=ALU.add),
                     reads=[sc2, fneg], writes=[nsl])
                ptr2 = ps_s.next()
                k.op("pe", lambda e: e.transpose(out=ptr2.ap[:, 0:128], in_=nsl.ap[:], identity=idf.ap[:]),
                     reads=[nsl, idf], writes=[ptr2])
                k.op("act", lambda e: e.copy(out=nselT.ap[:, s4 * 128:(s4 + 1) * 128], in_=ptr2.ap[:, 0:128]),
                     reads=[ptr2], writes=[nselT])
            for h in range(4):
                qtile = load_q(qbr, h * 128, ts)
                for br in range(2):
                    po = ps_o.next(); pden = ps_den.next()
                    pairs = []
                    if br == 0:
                        for kt in range(4 * qt + 4):
                            pairs.append((ksT.ap[:, kt * 128:(kt + 1) * 128], ksT, em.ap[:, kt, :], nselT.ap[:], (em, nselT),
                                          vsS.ap[:, kt, :], vsS))
                        for sq in range(4):
                            kt = 4 * qt + sq
                            pairs.append((ksT.ap[:, kt * 128:(kt + 1) * 128], ksT, ident, mk.ap[:, M_L + sq, :], (cs, mk),
                                          vsS.ap[:, kt, :], vsS))
                    else:
                        for a in range(4):
                            kt = 4 * qt - 4 + a
                            if kt < 0:
                                continue
                            pairs.append((kwT.ap[:, kt * 128:(kt + 1) * 128], kwT, ident, mk.ap[:, M_WH + a, :], (cs, mk),
                                          vwv[:, kt, :], vwS))
                        for a in range(4):
                            kt = 4 * qt + a
                            pairs.append((kwT.ap[:, kt * 128:(kt + 1) * 128], kwT, ident, mk.ap[:, M_WO + a, :], (cs, mk),
                                          vwv[:, kt, :], vwS))
                    attn_pairs(qtile, qtile.ap[:], pairs, SC128, po, pden, ones, True)
                    g = load_gate(h * 3 + 1 + br, ts)
                    rd = f32r.next()
                    k.op("dve", lambda e: e.reciprocal(out=rd.ap[:], in_=pden.ap[:]), reads=[pden], writes=[rd])
                    k.op("pool", lambda e: e.tensor_tensor(out=rd.ap[:], in0=rd.ap[:], in1=g.ap[:], op=ALU.mult),
                         reads=[rd, g], writes=[rd])
                    k.op("dve", lambda e: e.tensor_tensor(out=rd.ap[:], in0=po.ap[:], in1=rd.ap[:], op=ALU.mult),
                         reads=[po, rd], writes=[rd])
                    if br == 0:
                        k.op("pool", lambda e: e.tensor_tensor(out=acc[h].ap[:], in0=acc[h].ap[:], in1=rd.ap[:], op=ALU.add),
                             reads=[acc[h], rd], writes=[acc[h]])
                    else:
                        ot = o_ring.next()
                        k.op("pool", lambda e: e.tensor_tensor(out=ot.ap[:], in0=acc[h].ap[:], in1=rd.ap[:], op=ALU.add),
                             reads=[acc[h], rd], writes=[ot])
                        k.dma("pool", ob[h * 128:(h + 1) * 128, ts], ot.ap[:], reads=[ot], writes=[o_ob], sembuf=ot)
            for ch in range(4):
                qtile = load_q(qa, ch * 128, ts)
                po = ps_o.next(); pden = ps_den.next()
                for half in range(2):
                    rows = slice(half * 64, (half + 1) * 64)
                    vsl = slice(64, 192) if half == 0 else slice(0, 128)
                    pairs = []
                    if qt > 0:
                        kt = 4 * qt - 1
                        pairs.append((kaT.ap[rows, kt * 128:(kt + 1) * 128], kaT, ident, mk.ap[:, M_AH, :], (cs, mk),
                                      vaP.ap[:, kt, vsl], vaP))
                    for a in range(4):
                        kt = 4 * qt + a
                        pairs.append((kaT.ap[rows, kt * 128:(kt + 1) * 128], kaT, ident, mk.ap[:, M_AO + a, :], (cs, mk),
                                      vaP.ap[:, kt, vsl], vaP))
                    attn_pairs(qtile, qtile.ap[rows, :], pairs, SC64, po, pden, onesA if half == 0 else onesB, half == 0)
                rd = f32r.next()
                k.op("dve", lambda e: e.tensor_scalar(out=rd.ap[:], in0=pden.ap[:], scalar1=esink.ap[:, ch:ch + 1], scalar2=None,
                                                      op0=ALU.add), reads=[pden, esink], writes=[rd])
                k.op("dve", lambda e: e.reciprocal(out=rd.ap[:], in_=rd.ap[:]), reads=[rd], writes=[rd])
                ot = o_ring.next()
                k.op("dve", lambda e: e.tensor_tensor(out=ot.ap[:], in0=po.ap[:], in1=rd.ap[:], op=ALU.mult),
                     reads=[po, rd], writes=[ot])
                k.dma("pool", oa[ch * 128:(ch + 1) * 128, ts], ot.ap[:], reads=[ot], writes=[o_oa], sembuf=ot)
        k.finish([o_oa, o_ob] + o_ring.bufs)
        stats = dict(k.cnt)
    return nc, stats


def _consts_B():
    bf = lambda a: np.ascontiguousarray(a.astype(NPBF))
    p = np.arange(128)[:, None]
    col = np.arange(512)[None, :]
    m = np.zeros((NM, 128, 512), np.float32)
    for rel in range(5):
        m[M_P + rel] = np.where(16 * p + 31 <= 512 * rel + col, 0.0, NEG)
    for rel in range(4):
        m[M_P3 + rel] = np.where(16 * p + 31 <= 512 * rel + col, 0.0, NEG)
        m[M_P3 + rel][127, :] = NEG
    for a in range(4):
        s = a * 128 + p
        m[M_L + a] = np.where((s // 64 == col // 64) & (col >= s), 0.0, NEG)
        m[M_WO + a] = np.where(col >= s, 0.0, NEG)
        m[M_WH + a] = np.where(col < s, 0.0, NEG)
        m[M_AO + a] = np.where((col >= s) & (col - s < 128), 0.0, NEG)
    m[M_AH] = np.where(col < p, 0.0, NEG)
    masks = bf(np.ascontiguousarray(m.transpose(1, 0, 2)).reshape(128, NM * 512))
    ncb, ns = 511, 128
    a_, b_ = 4, 2
    j = np.arange(ns)[:, None, None]
    c = a_ * j - np.arange(a_)[None, :, None] - np.arange(b_)[None, None, :]
    jj = np.broadcast_to(j, c.shape)
    ok = (c >= 0) & (c < ncb)
    mat = np.zeros((512, ns), np.float32)
    np.add.at(mat, (c[ok], jj[ok]), 1.0)
    selmap = bf(np.ascontiguousarray(mat.reshape(4, 128, 128).transpose(1, 0, 2)).reshape(128, 512))
    t = np.arange(SEQ)[:, None]
    jb = np.arange(128)[None, :]
    cur = t // 64
    selbias = np.where(jb * 64 <= t, 0.0, -1.0e30).astype(np.float32)
    selbias = np.where(jb == cur - 1, 1.0e30, selbias)
    selbias = np.where(jb == cur, 2.0e30, selbias)
    selbias = np.where(jb == 0, 3.0e30, selbias).astype(np.float32)
    farneg = np.where(jb <= cur - 1, 0.0, NEG).astype(np.float32)
    e = np.zeros((128, 64, 128), np.float32)
    for kt in range(64):
        for pp in range(128):
            e[2 * kt + pp // 64, kt, pp] = 1.0
    emat = bf(e.reshape(128, 64 * 128))
    cst = np.zeros((128, 448), np.float32)
    cst[:, 0:128] = np.eye(128)
    cst[:, 128:256] = 1.0
    cst[:, 320:384] = 1.0
    identf = np.eye(128, dtype=np.float32)
    return dict(masks=masks, selmap=selmap, selbias=selbias, farneg=farneg, emat=emat, cst=bf(cst), identf=identf)


def build_C(ntiles=NTOK // TT):
    nc = bass.Bass("TRN2", target_bir_lowering=False)
    ntok = ntiles * TT
    dt_ = lambda n, s, d, kind="ExternalInput": nc.dram_tensor(n, list(s), d, kind=kind).ap()
    h1T = dt_("h1T", [D, ntok], F32)
    oaT = dt_("oaT", [2048, ntok], BF16)
    obT = dt_("obT", [2048, ntok], BF16)
    pT = dt_("pT", [256, ntok], F32)
    gm = dt_("gm", [128, 32], F32)
    g2 = dt_("g2", [128, 32], F32)
    gp = dt_("gp", [128, 32], F32)
    gf = dt_("gf", [128, 32], F32)
    win = dt_("win", [D, 14384], F32)
    woa = dt_("woa", [2048, D], F32)
    wob = dt_("wob", [2048, D], F32)
    wo = dt_("wo", [D, D], F32)
    wg = dt_("wg", [D, DFF], F32)
    wu = dt_("wu", [D, DFF], F32)
    wd = dt_("wd", [DFF, D], F32)
    wpg = dt_("wpg", [D, D], F32)
    wpp = dt_("wpp", [256, D], F32)
    outT = dt_("outT", [D, ntok], F32, "ExternalOutput")

    with ExitStack() as es:
        k = K(nc, es)
        hT = k.sb([128, 32, TT], F32, "hT")
        hbufs = [k.buf(hT[:, c, :]) for c in range(32)]
        xn = k.sb([128, 32, TT], BF16, "xn")
        xnb = [k.buf(xn[:, c, :]) for c in range(32)]
        slots = [k.buf(k.sb([128, 8192], BF16, f"w{i}"), f"w{i}") for i in range(4)]
        ring2 = Ring.__new__(Ring); ring2.bufs = slots[0:2]; ring2.i = 0
        ring4 = Ring.__new__(Ring); ring4.bufs = slots; ring4.i = 0
        oa_b, ob_b = slots[2], slots[3]
        oav = oa_b.ap[:].rearrange("p (c t) -> p c t", t=TT)
        obv = ob_b.ap[:].rearrange("p (c t) -> p c t", t=TT)
        ones_f = k.buf(k.sb([128, 128], F32, "ones"))
        gb_ = {}
        for nm, src in (("gm", gm), ("g2", g2), ("gp", gp), ("gf", gf)):
            gb_[nm] = k.buf(k.sb([128, 32], F32, nm + "s"))
        f32r = Ring(k, 6, [128, TT], F32, "f32r")
        rstd = k.sb([128, TT], F32, "rstd"); rstd_b = k.buf(rstd)
        tmp = k.sb([128, TT], F32, "tmp"); tmp_b = k.buf(tmp)
        hid_ring = Ring(k, 4, [128, TT], BF16, "hid")
        pTs = k.buf(k.sb([128, 2, TT], BF16, "pTs"))
        wpp_b = k.buf(k.sb([128, 2 * D], BF16, "wppb"))
        ps_g = PsRing(k, 2, "psg")
        ps_u = PsRing(k, 2, "psu")
        ps_d = PsRing(k, 4, "psd")
        scr = (f32r, rstd, rstd_b, tmp, tmp_b)
        o_out = k.buf(outT)

        k.op("dve", lambda e: e.memset(ones_f.ap[:], 1.0), writes=[ones_f])
        for nm, src in (("gm", gm), ("g2", g2), ("gp", gp), ("gf", gf)):
            k.dma("sp", gb_[nm].ap[:], src, writes=[gb_[nm]])

        for t in range(ntiles):
            ts = slice(t * TT, (t + 1) * TT)
            for c4 in range(4):
                k.dma("sp", hT[:, c4 * 8:(c4 + 1) * 8, :],
                      h1T[c4 * 1024:(c4 + 1) * 1024, ts].rearrange("(c p) t -> p c t", p=128),
                      writes=hbufs[c4 * 8:(c4 + 1) * 8], sembuf=hbufs[c4 * 8])
            k.dma("sp", oav, oaT[:, ts].rearrange("(c p) t -> p c t", p=128), writes=[oa_b])
            k.dma("sp", obv, obT[:, ts].rearrange("(c p) t -> p c t", p=128), writes=[ob_b])
            k.dma("pool", pTs.ap[:], pT[:, ts].rearrange("(c p) t -> p c t", p=128), writes=[pTs])
            _norm_to_bf16(k, hbufs, hT, gb_["gm"], xn, xnb, ones_f, ps_d, scr)
            for b in range(16):
                c0 = b * 256
                blk = {}
                for nm, src, col0, kc_ in (("ga", win, 6192 + c0, 32), ("gb", win, 10288 + c0, 32),
                                           ("oa", woa, c0, 16), ("ob", wob, c0, 16)):
                    wb = ring2.next()
                    wv = wb.ap[:, 0:kc_ * 256].rearrange("p (c f) -> p c f", f=256)
                    k.dma("pool", wv, src[:, col0:col0 + 256].rearrange("(c p) f -> p c f", p=128), writes=[wb])
                    blk[nm] = (wb, wv)
                    if nm == "gb":
                        sig = {}
                        for gn, psr in (("ga", ps_g), ("gb", ps_u)):
                            wb_, wv_ = blk[gn]
                            for j in range(2):
                                pp = psr.next()
                                for c in range(32):
                                    k.op("pe", lambda e: e.matmul(pp.ap[:], lhsT=wv_[:, c, j * 128:(j + 1) * 128], rhs=xn[:, c, :],
                                                                  start=(c == 0), stop=(c == 31)), reads=[wb_, xnb[c]], writes=[pp])
                                sg = f32r.next()
                                k.op("act", lambda e: e.activation(out=sg.ap[:], in_=pp.ap[:], func=AF.Sigmoid),
                                     reads=[pp], writes=[sg])
                                sig[(gn, j)] = sg
                mgs = []
                for j in range(2):
                    terms = []
                    for on, gn, ov, obuf in (("oa", "ga", oav, oa_b), ("ob", "gb", obv, ob_b)):
                        wb_, wv_ = blk[on]
                        pp = ps_d.next()
                        for c in range(16):
                            k.op("pe", lambda e: e.matmul(pp.ap[:], lhsT=wv_[:, c, j * 128:(j + 1) * 128], rhs=ov[:, c, :],
                                                          start=(c == 0), stop=(c == 15)), reads=[wb_, obuf], writes=[pp])
                        sg = sig[(gn, j)]
                        k.op("dve", lambda e: e.tensor_tensor(out=sg.ap[:], in0=pp.ap[:], in1=sg.ap[:], op=ALU.mult),
                             reads=[pp, sg], writes=[sg])
                        terms.append(sg)
                    mg = hid_ring.next()
                    k.op("pool", lambda e: e.tensor_tensor(out=mg.ap[:], in0=terms[0].ap[:], in1=terms[1].ap[:], op=ALU.add),
                         reads=terms, writes=[mg])
                    mgs.append(mg)
                wb = ring2.next()
                wv = wb.ap[:].rearrange("p (c f) -> p c f", f=D)
                k.dma("pool", wv, wo[c0:c0 + 256, :].rearrange("(c p) f -> p c f", p=128), writes=[wb])
                for dc in range(32):
                    pd = ps_d.next()
                    for j in range(2):
                        k.op("pe", lambda e: e.matmul(pd.ap[:], lhsT=wv[:, j, dc * 128:(dc + 1) * 128], rhs=mgs[j].ap[:],
                                                      start=(j == 0), stop=(j == 1)), reads=[wb, mgs[j]], writes=[pd])
                    k.op("dve", lambda e: e.tensor_tensor(out=hT[:, dc, :], in0=pd.ap[:], in1=hT[:, dc, :], op=ALU.add),
                         reads=[pd, hbufs[dc]], writes=[hbufs[dc]])
            _norm_to_bf16(k, hbufs, hT, gb_["g2"], xn, xnb, ones_f, ps_d, scr)
            _ffn(k, hT, hbufs, xn, xnb, wg, wu, wd, ring4, ps_g, ps_u, ps_d, hid_ring, f32r)
            _norm_to_bf16(k, hbufs, hT, gb_["gp"], xn, xnb, ones_f, ps_d, scr)
            wpb = wpp_b
            wpv = wpb.ap[:].rearrange("p (c f) -> p c f", f=D)
            k.dma("pool", wpv, wpp.rearrange("(c p) f -> p c f", p=128), writes=[wpb])
            for b in range(16):
                wb = ring4.next()
                wv = wb.ap[:].rearrange("p (c f) -> p c f", f=256)
                k.dma("pool", wv, wpg[:, b * 256:(b + 1) * 256].rearrange("(c p) f -> p c f", p=128), writes=[wb])
                for j in range(2):
                    dc = b * 2 + j
                    pp = ps_g.next()
                    for c in range(32):
                        k.op("pe", lambda e: e.matmul(pp.ap[:], lhsT=wv[:, c, j * 128:(j + 1) * 128], rhs=xn[:, c, :],
                                                      start=(c == 0), stop=(c == 31)), reads=[wb, xnb[c]], writes=[pp])
                    sg = f32r.next()
                    k.op("act", lambda e: e.activation(out=sg.ap[:], in_=pp.ap[:], func=AF.Sigmoid), reads=[pp], writes=[sg])
                    pq = ps_u.next()
                    for c in range(2):
                        k.op("pe", lambda e: e.matmul(pq.ap[:], lhsT=wpv[:, c, dc * 128:(dc + 1) * 128], rhs=pTs.ap[:, c, :],
                                                      start=(c == 0), stop=(c == 1)), reads=[wpb, pTs], writes=[pq])
                    k.op("dve", lambda e: e.tensor_tensor(out=sg.ap[:], in0=pq.ap[:], in1=sg.ap[:], op=ALU.mult),
                         reads=[pq, sg], writes=[sg])
                    k.op("pool", lambda e: e.tensor_tensor(out=hT[:, dc, :], in0=hT[:, dc, :], in1=sg.ap[:], op=ALU.add),
                         reads=[sg, hbufs[dc]], writes=[hbufs[dc]])
            pss = ps_d.next()
            for c in range(32):
                sq = f32r.next()
                k.op("act", lambda e: e.activation(out=sq.ap[:], in_=hT[:, c, :], func=AF.Square), reads=[hbufs[c]], writes=[sq])
                k.op("pe", lambda e: e.matmul(pss.ap[:], lhsT=ones_f.ap[:], rhs=sq.ap[:], start=(c == 0), stop=(c == 31)),
                     reads=[sq, ones_f], writes=[pss])
            k.op("dve", lambda e: e.tensor_scalar(out=tmp[:], in0=pss.ap[:], scalar1=1.0 / D, scalar2=EPS,
                                                  op0=ALU.mult, op1=ALU.add), reads=[pss], writes=[tmp_b])
            k.op("act", lambda e: e.activation(out=tmp[:], in_=tmp[:], func=AF.Sqrt), reads=[tmp_b], writes=[tmp_b])
            k.op("dve", lambda e: e.reciprocal(out=rstd[:], in_=tmp[:]), reads=[tmp_b], writes=[rstd_b])
            for c in range(32):
                k.op("dve", lambda e: e.scalar_tensor_tensor(out=hT[:, c, :], in0=hT[:, c, :], scalar=gb_["gf"].ap[:, c:c + 1],
                                                             in1=rstd[:], op0=ALU.mult, op1=ALU.mult),
                     reads=[hbufs[c], gb_["gf"], rstd_b], writes=[hbufs[c]])
            for c4 in range(4):
                k.dma("sp", outT[c4 * 1024:(c4 + 1) * 1024, ts].rearrange("(c p) t -> p c t", p=128),
                      hT[:, c4 * 8:(c4 + 1) * 8, :], reads=hbufs[c4 * 8:(c4 + 1) * 8], writes=[o_out], sembuf=o_out)
        k.finish([o_out])
        stats = dict(k.cnt)
    return nc, stats


_CACHE = {}


def _get(name, fn):
    if name not in _CACHE:
        _CACHE[name] = fn()[0]
    return _CACHE[name]


def kernel(x, p, positions, ffn1_norm, ffn1_w_gate, ffn1_w_up, ffn1_w_down, mix_norm, w_in, sinks,
           nsa_pe_k, nsa_w_ck1, nsa_w_ck2, nsa_pe_v, nsa_w_cv1, nsa_w_cv2, w_o_a, w_o_b, w_o,
           ffn2_norm, ffn2_w_gate, ffn2_w_up, ffn2_w_down, ple_norm, w_ple_gate, w_ple_proj, final_norm):
    f32 = lambda a: np.ascontiguousarray(np.asarray(a), dtype=np.float32)
    x = np.asarray(x); p = np.asarray(p); positions = np.asarray(positions)
    cores = list(range(NCORE))
    tok = lambda c: (c // 4, slice((c % 4) * NTOK, (c % 4 + 1) * NTOK))
    ncA = _get("A", build_A)
    rmat, invf = _consts_A()
    shared = dict(g1=gcol(f32(ffn1_norm)[0]), g2=gcol(f32(mix_norm)[0]), wg=f32(ffn1_w_gate)[0], wu=f32(ffn1_w_up)[0],
                  wd=f32(ffn1_w_down)[0], win=f32(w_in)[0], rmat=rmat, invf=invf)
    in_maps = []
    for c in cores:
        b, sl = tok(c)
        in_maps.append(dict(xT=np.ascontiguousarray(x[b, sl].T.astype(np.float32)),
                            pos=np.ascontiguousarray(positions[b:b + 1, sl].astype(np.int32)), **shared))
    resA = run_bass_kernel_spmd(ncA, in_maps, core_ids=cores).results
    cat = lambda key, b: np.concatenate([np.asarray(resA[4 * b + kq][key]) for kq in range(4)], axis=1)
    ncB = _get("B", build_B)
    cB = _consts_B()
    sk = f32(sinks)[0]
    in_maps = []
    for b in range(2):
        qaT = cat("qaT", b); qbT = cat("qbT", b); qbrT = cat("qbrT", b); kT = cat("kT", b); gbT = cat("gbT", b)
        vt = np.concatenate([np.asarray(resA[4 * b + kq]["vtok"]) for kq in range(4)], axis=0)
        for r in range(4):
            g = r // 2
            own = list(range(4 * r, 4 * r + 4))
            oth = [h for h in range(8 * g, 8 * g + 8) if h not in own]
            esk = np.zeros((128, 4), np.float32)
            for ch in range(4):
                esk[0:64, ch] = sk[8 * r + 2 * ch]
                esk[64:128, ch] = sk[8 * r + 2 * ch + 1]
            ka = kT[64 * r:64 * r + 64]
            rows = lambda a, h: a[128 * h:128 * h + 128]
            in_maps.append(dict(
                qa=np.ascontiguousarray(qaT[512 * r:512 * r + 512]),
                ka2=np.ascontiguousarray(np.concatenate([ka, ka], axis=0)),
                va=np.ascontiguousarray(vt[:, 64 * r:64 * r + 64]),
                esk=esk,
                qb=np.ascontiguousarray(np.concatenate([rows(qbT, h) for h in own + oth], axis=0)),
                qbr=np.ascontiguousarray(np.concatenate([rows(qbrT, h) for h in own], axis=0)),
                kc=np.ascontiguousarray(kT[256 + 128 * g:256 + 128 * g + 128]),
                vc=np.ascontiguousarray(kT[512 + 128 * g:512 + 128 * g + 128]),
                ks=np.ascontiguousarray(kT[768 + 128 * g:768 + 128 * g + 128]),
                kw=np.ascontiguousarray(kT[1024 + 128 * g:1024 + 128 * g + 128]),
                vs=np.ascontiguousarray(vt[:, 256 + 128 * g:256 + 128 * g + 128]),
                vw=np.ascontiguousarray(vt[:, 512 + 128 * g:512 + 128 * g + 128]),
                gb=np.ascontiguousarray(gbT[12 * r:12 * r + 12]),
                pekT=np.ascontiguousarray(f32(nsa_pe_k)[0].T), pevT=np.ascontiguousarray(f32(nsa_pe_v)[0].T),
                wk1=f32(nsa_w_ck1)[0], wk2=f32(nsa_w_ck2)[0], wv1=f32(nsa_w_cv1)[0], wv2=f32(nsa_w_cv2)[0], **cB))
    resB = run_bass_kernel_spmd(ncB, in_maps, core_ids=cores).results
    ncC = _get("C", build_C)
    shared = dict(gm=gcol(f32(mix_norm)[0]), g2=gcol(f32(ffn2_norm)[0]), gp=gcol(f32(ple_norm)[0]), gf=gcol(f32(final_norm)),
                  win=f32(w_in)[0], woa=f32(w_o_a)[0], wob=f32(w_o_b)[0], wo=f32(w_o)[0], wg=f32(ffn2_w_gate)[0],
                  wu=f32(ffn2_w_up)[0], wd=f32(ffn2_w_down)[0], wpg=f32(w_ple_gate)[0], wpp=f32(w_ple_proj)[0])
    oaF = [np.concatenate([np.asarray(resB[4 * b + r]["oa"]) for r in range(4)], axis=0) for b in range(2)]
    obF = [np.concatenate([np.asarray(resB[4 * b + r]["ob"]) for r in range(4)], axis=0) for b in range(2)]
    in_maps = []
    for c in cores:
        b, sl = tok(c)
        in_maps.append(dict(h1T=np.asarray(resA[c]["h1T"]), oaT=np.ascontiguousarray(oaF[b][:, sl]),
                            obT=np.ascontiguousarray(obF[b][:, sl]),
                            pT=np.ascontiguousarray(p[0, b, sl].T.astype(np.float32)), **shared))
    resC = run_bass_kernel_spmd(ncC, in_maps, core_ids=cores).results
    out = np.empty((2, SEQ, D), np.float32)
    for c in cores:
        b, sl = tok(c)
        out[b, sl] = np.asarray(resC[c]["outT"]).T
    return out
```
